# Optimizing a Trainium2 kernel written in Bass

```python
import jax, jax.numpy as jnp
from jax import lax
import numpy as np

D_MODEL = 2048
BATCH = 2
SEQ = 16384
DEPTH = 1

GRID_W = 64
N_META = 16
EPS = 1e-6
NEG_INF = -1e30
NA_HEADS = 8
NA_HEAD_DIM = 128
NA_WIN_ROWS = 8
NA_WIN_COLS = 16
NA_COL_BLOCK = 16
NA_KEY_COLS = NA_COL_BLOCK + NA_WIN_COLS
MLA_HEADS = 8
MLA_Q_RANK = 512
MLA_KV_RANK = 256
MLA_NOPE_DIM = 128
MLA_ROPE_DIM = 64
MLA_V_DIM = 128
MLA_QK_DIM = MLA_NOPE_DIM + MLA_ROPE_DIM
ROPE_THETA = 10000.0
Q_BLOCK = 128
D_FF = 5632
CONV_W = 3

NA_WIDTH = NA_HEADS * NA_HEAD_DIM
MLA_WIDTH = MLA_HEADS * MLA_V_DIM
D_MIX = NA_WIDTH + MLA_WIDTH
IN_COLS = 3 * NA_WIDTH + MLA_Q_RANK + MLA_KV_RANK + MLA_ROPE_DIM
IN_SPLITS = [NA_WIDTH, 2 * NA_WIDTH, 3 * NA_WIDTH, 3 * NA_WIDTH + MLA_Q_RANK, 3 * NA_WIDTH + MLA_Q_RANK + MLA_KV_RANK]

kernel_name = "hybrid_na_mla_convffn_encoder"


def rmsnorm(x, g):
    xf = x.astype(jnp.float32)
    y = xf * lax.rsqrt(jnp.mean(xf * xf, axis=-1, keepdims=True) + EPS)
    return (y * g.astype(jnp.float32)).astype(x.dtype)


def rope_tables(length, dtype):
    inv = ROPE_THETA ** (-jnp.arange(0, MLA_ROPE_DIM, 2, dtype=jnp.float32) / MLA_ROPE_DIM)
    ang = jnp.arange(length, dtype=jnp.float32)[:, None] * inv[None, :]
    return jnp.cos(ang)[:, None, :].astype(dtype), jnp.sin(ang)[:, None, :].astype(dtype)


def apply_rope(x, cos, sin):
    half = x.shape[-1] // 2
    x1, x2 = x[..., :half], x[..., half:]
    return jnp.concatenate([x1 * cos - x2 * sin, x2 * cos + x1 * sin], axis=-1)


def neighbourhood_attention(q, k, v, rpb, meta_bias):
    B, L, H, D = q.shape
    T = L - N_META
    rows = T // GRID_W
    kh = min(NA_WIN_ROWS, rows)
    scale = D ** -0.5
    n_cb = GRID_W // NA_COL_BLOCK
    qm, km, vm = q[:, :N_META], k[:, :N_META], v[:, :N_META]

    s_mm = jnp.einsum('bqhd,bkhd->bhqk', qm, km).astype(jnp.float32) * scale + meta_bias.astype(jnp.float32)[None, :, None, :]
    p_mm = jax.nn.softmax(s_mm, axis=-1).astype(vm.dtype)
    meta_out = jnp.einsum('bhqk,bkhd->bqhd', p_mm, vm)

    qg = q[:, N_META:].reshape(B, rows, GRID_W, H, D)
    kg = k[:, N_META:].reshape(B, rows, GRID_W, H, D)
    vg = v[:, N_META:].reshape(B, rows, GRID_W, H, D)

    qcols = np.arange(GRID_W).reshape(n_cb, NA_COL_BLOCK)
    col_start = np.clip(qcols - NA_WIN_COLS // 2, 0, GRID_W - NA_WIN_COLS)
    kblk_start = np.clip(np.arange(n_cb) * NA_COL_BLOCK - NA_WIN_COLS // 2, 0, GRID_W - NA_KEY_COLS)
    key_cols = kblk_start[:, None] + np.arange(NA_KEY_COLS)[None, :]
    col_valid = (key_cols[:, None, :] >= col_start[:, :, None]) & (key_cols[:, None, :] < col_start[:, :, None] + NA_WIN_COLS)
    col_mask = jnp.asarray(np.where(col_valid, 0.0, NEG_INF), jnp.float32)
    dc_idx = np.clip(key_cols[:, None, :] - qcols[:, :, None] + NA_WIN_COLS - 1, 0, 2 * NA_WIN_COLS - 2)
    rpb32 = rpb.astype(jnp.float32)
    mbias = meta_bias.astype(jnp.float32)[None, None, None]

    def row_block(r):
        rs = jnp.clip(r - kh // 2, 0, rows - kh)
        kb = lax.dynamic_slice_in_dim(kg, rs, kh, axis=1)[:, :, key_cols]
        vb = lax.dynamic_slice_in_dim(vg, rs, kh, axis=1)[:, :, key_cols]
        qr = lax.dynamic_index_in_dim(qg, r, axis=1, keepdims=False).reshape(B, n_cb, NA_COL_BLOCK, H, D)
        s_win = jnp.einsum('bjqhd,bijkhd->bjqhik', qr, kb).astype(jnp.float32) * scale
        dr_idx = rs + jnp.arange(kh) - r + NA_WIN_ROWS - 1
        bias = rpb32[:, dr_idx][:, :, dc_idx].transpose(2, 3, 0, 1, 4)
        s_win = s_win + bias[None] + col_mask[None, :, :, None, None, :]
        s_met = jnp.einsum('bjqhd,bmhd->bjqhm', qr, km).astype(jnp.float32) * scale + mbias
        s = jnp.concatenate([s_met, s_win.reshape(B, n_cb, NA_COL_BLOCK, H, kh * NA_KEY_COLS)], axis=-1)
        p = jax.nn.softmax(s, axis=-1).astype(vb.dtype)
        p_win = p[..., N_META:].reshape(B, n_cb, NA_COL_BLOCK, H, kh, NA_KEY_COLS)
        out = jnp.einsum('bjqhm,bmhd->bjqhd', p[..., :N_META], vm) + jnp.einsum('bjqhik,bijkhd->bjqhd', p_win, vb)
        return out.reshape(B, GRID_W, H, D)

    real = lax.map(row_block, jnp.arange(rows))
    real = real.transpose(1, 0, 2, 3, 4).reshape(B, T, H, D)
    return jnp.concatenate([meta_out, real], axis=1)


def mla_attention(cq, ckv, k_pe, cq_g, ckv_g, w_q_up, w_kv_up, q_g, k_g, cos, sin):
    B, L, _ = cq.shape
    q = (rmsnorm(cq, cq_g) @ w_q_up).reshape(B, L, MLA_HEADS, MLA_QK_DIM)
    kv = (rmsnorm(ckv, ckv_g) @ w_kv_up).reshape(B, L, MLA_HEADS, MLA_NOPE_DIM + MLA_V_DIM)
    k_nope, v = kv[..., :MLA_NOPE_DIM], kv[..., MLA_NOPE_DIM:]
    k = jnp.concatenate([k_nope, jnp.broadcast_to(k_pe[:, :, None, :], (B, L, MLA_HEADS, MLA_ROPE_DIM))], axis=-1)
    q = rmsnorm(q, q_g)
    k = rmsnorm(k, k_g)
    q = jnp.concatenate([q[..., :MLA_NOPE_DIM], apply_rope(q[..., MLA_NOPE_DIM:], cos, sin)], axis=-1)
    k = jnp.concatenate([k[..., :MLA_NOPE_DIM], apply_rope(k[..., MLA_NOPE_DIM:], cos, sin)], axis=-1)
    scale = MLA_QK_DIM ** -0.5

    def attend(qb):
        s = jnp.einsum('bqhd,bkhd->bhqk', qb, k).astype(jnp.float32) * scale
        p = jax.nn.softmax(s, axis=-1).astype(v.dtype)
        return jnp.einsum('bhqk,bkhd->bqhd', p, v)

    out_meta = attend(q[:, :N_META])
    nb = (L - N_META) // Q_BLOCK
    qr = q[:, N_META:].reshape(B, nb, Q_BLOCK, MLA_HEADS, MLA_QK_DIM).transpose(1, 0, 2, 3, 4)
    out_real = lax.map(attend, qr).transpose(1, 0, 2, 3, 4).reshape(B, L - N_META, MLA_HEADS, MLA_V_DIM)
    return jnp.concatenate([out_meta, out_real], axis=1)


def conv_glu(x, w_gate, w_up, conv_w, conv_b, w_down):
    L = x.shape[1]
    g = x @ w_gate
    u = x @ w_up
    pad = CONV_W // 2
    gp = jnp.pad(g, ((0, 0), (pad, pad), (0, 0)))
    gc = conv_b
    for i in range(CONV_W):
        gc = gc + conv_w[i] * gp[:, i:i + L]
    return (jax.nn.silu(gc) * u) @ w_down


def setup_inputs(seed: int = 0) -> dict:
    key = jax.random.key(seed)
    ks = jax.random.split(key, 24)
    f32 = jnp.float32

    def nrm(k, shape, s):
        return jax.random.normal(k, shape, f32) * s

    def gain(k, shape):
        return 1.0 + 0.05 * jax.random.normal(k, shape, f32)

    return {
        "x": nrm(ks[0], (BATCH, SEQ, D_MODEL), 1.0),
        "meta_tokens": nrm(ks[1], (N_META, D_MODEL), 1.0),
        "mix_norm_g": gain(ks[2], (DEPTH, D_MODEL)),
        "w_in": nrm(ks[3], (DEPTH, D_MODEL, IN_COLS), D_MODEL ** -0.5),
        "na_q_g": gain(ks[4], (DEPTH, NA_HEAD_DIM)),
        "na_k_g": gain(ks[5], (DEPTH, NA_HEAD_DIM)),
        "na_rpb": nrm(ks[6], (DEPTH, NA_HEADS, 2 * NA_WIN_ROWS - 1, 2 * NA_WIN_COLS - 1), 0.1),
        "na_meta_bias": nrm(ks[7], (DEPTH, NA_HEADS, N_META), 0.1),
        "mla_cq_g": gain(ks[8], (DEPTH, MLA_Q_RANK)),
        "mla_ckv_g": gain(ks[9], (DEPTH, MLA_KV_RANK)),
        "w_q_up": nrm(ks[10], (DEPTH, MLA_Q_RANK, MLA_HEADS * MLA_QK_DIM), MLA_Q_RANK ** -0.5),
        "w_kv_up": nrm(ks[11], (DEPTH, MLA_KV_RANK, MLA_HEADS * (MLA_NOPE_DIM + MLA_V_DIM)), MLA_KV_RANK ** -0.5),
        "mla_q_g": gain(ks[12], (DEPTH, MLA_QK_DIM)),
        "mla_k_g": gain(ks[13], (DEPTH, MLA_QK_DIM)),
        "na_out_g": gain(ks[14], (DEPTH, NA_WIDTH)),
        "mla_out_g": gain(ks[15], (DEPTH, MLA_WIDTH)),
        "w_out": nrm(ks[16], (DEPTH, D_MIX, D_MODEL), D_MIX ** -0.5),
        "ffn_norm_g": gain(ks[17], (DEPTH, D_MODEL)),
        "w_gate": nrm(ks[18], (DEPTH, D_MODEL, D_FF), D_MODEL ** -0.5),
        "w_up": nrm(ks[19], (DEPTH, D_MODEL, D_FF), D_MODEL ** -0.5),
        "conv_w": nrm(ks[20], (DEPTH, CONV_W, D_FF), CONV_W ** -0.5),
        "conv_b": nrm(ks[21], (DEPTH, D_FF), 0.01),
        "w_down": nrm(ks[22], (DEPTH, D_FF, D_MODEL), D_FF ** -0.5),
    }


def reference(x, meta_tokens, mix_norm_g, w_in, na_q_g, na_k_g, na_rpb, na_meta_bias, mla_cq_g, mla_ckv_g,
              w_q_up, w_kv_up, mla_q_g, mla_k_g, na_out_g, mla_out_g, w_out, ffn_norm_g, w_gate, w_up,
              conv_w, conv_b, w_down):
    B = x.shape[0]
    meta = jnp.broadcast_to(meta_tokens.astype(x.dtype)[None], (B, N_META, D_MODEL))
    h = jnp.concatenate([meta, x], axis=1)
    L = h.shape[1]
    cos, sin = rope_tables(L, x.dtype)
    for l in range(DEPTH):
        hn = rmsnorm(h, mix_norm_g[l])
        proj = hn @ w_in[l]
        q_a, k_a, v_a, cq, ckv, k_pe = jnp.split(proj, IN_SPLITS, axis=-1)
        q_a = rmsnorm(q_a.reshape(B, L, NA_HEADS, NA_HEAD_DIM), na_q_g[l])
        k_a = rmsnorm(k_a.reshape(B, L, NA_HEADS, NA_HEAD_DIM), na_k_g[l])
        v_a = v_a.reshape(B, L, NA_HEADS, NA_HEAD_DIM)
        out_a = neighbourhood_attention(q_a, k_a, v_a, na_rpb[l], na_meta_bias[l]).reshape(B, L, NA_WIDTH)
        out_b = mla_attention(cq, ckv, k_pe, mla_cq_g[l], mla_ckv_g[l], w_q_up[l], w_kv_up[l],
                              mla_q_g[l], mla_k_g[l], cos, sin).reshape(B, L, MLA_WIDTH)
        mix = jnp.concatenate([rmsnorm(out_a, na_out_g[l]), rmsnorm(out_b, mla_out_g[l])], axis=-1)
        h = h + mix @ w_out[l]
        h = h + conv_glu(rmsnorm(h, ffn_norm_g[l]), w_gate[l], w_up[l], conv_w[l], conv_b[l], w_down[l])
    return h[:, N_META:]
```

```python
import bisect
import contextlib
import numpy as np
import ml_dtypes
import concourse.bass as bass
import concourse.mybir as mybir
from concourse.bass_utils import run_bass_kernel_spmd

F32 = mybir.dt.float32
BF16 = mybir.dt.bfloat16
AF = mybir.ActivationFunctionType
ALU = mybir.AluOpType

D = 2048
SEQ = 16384
NMETA = 16
L = SEQ + NMETA
LP = 16512
NCH = 129
EXTG = 4736
EXT = EXTG + 16
DFF = 5632
NFC = 44
EPS = 1e-6
MASKV = -1.0e4
SEM_LIMIT = 30000
SAME_ENGINE_SYNC = True
DBG = {}
MIX0 = 256
NMIX = 4224


class Prog:
    def __init__(self):
        self.ops = []
        self.lastw = {}
        self.lastr = {}
        self.last_dma = {}

    def add(self, eng, fn, reads=(), writes=(), dma=None, n=1, extra=None):
        idx = len(self.ops)
        stream = ("dma", dma) if dma is not None else eng
        deps = {}

        def dep(s, i):
            if s == stream and dma is None and (eng in ("pe", "sp") or not SAME_ENGINE_SYNC):
                return
            if deps.get(s, -1) < i:
                deps[s] = i

        for k in reads:
            w = self.lastw.get(k)
            if w is not None:
                dep(*w)
        for k in writes:
            w = self.lastw.get(k)
            if w is not None:
                dep(*w)
            for s, i in self.lastr.get(k, {}).items():
                dep(s, i)
        if extra:
            for s, i in extra.items():
                dep(s, i)
        for k in reads:
            self.lastr.setdefault(k, {})[stream] = idx
        for k in writes:
            self.lastw[k] = (stream, idx)
            self.lastr[k] = {}
        if dma is not None:
            self.last_dma[dma] = idx
        self.ops.append((eng, fn, deps, dma, n))
        return idx

    def emit(self, nc):
        ops = self.ops
        needs_inc = set()
        for (eng, fn, deps, dma, n) in ops:
            for s, i in deps.items():
                if not (isinstance(s, tuple) and s[0] == "dma"):
                    needs_inc.add(i)
        ticket = {}
        cnt = {}
        dma_list = {}
        for idx, (eng, fn, deps, dma, n) in enumerate(ops):
            if dma is not None:
                lst = dma_list.setdefault(dma, [])
                prev = lst[-1][1] if lst else 0
                lst.append((idx, prev + n))
            elif idx in needs_inc:
                cnt[eng] = cnt.get(eng, 0) + 1
                ticket[idx] = cnt[eng]
        dma_idx = {k: [a for a, _ in v] for k, v in dma_list.items()}
        dma_cum = {k: dict(v) for k, v in dma_list.items()}
        DL = SEM_LIMIT // 16
        needed = []
        for e, c in cnt.items():
            for b in range((c - 1) // SEM_LIMIT + 1):
                needed.append((e, b))
        for k, v in dma_list.items():
            tot = v[-1][1]
            for b in range((tot - 1) // DL + 1):
                needed.append((("dma", k), b))
        self.n_sems = len(needed)
        sems = {}
        per_eng = {}
        for idx, op in enumerate(ops):
            per_eng.setdefault(op[0], []).append(idx)
        with contextlib.ExitStack() as stack:
            for i, key in enumerate(needed):
                sems[key] = stack.enter_context(nc.semaphore("sm%d" % i))
            block = stack.enter_context(nc.Block())

            def run_engine(e, eng):
                waited = {}
                for idx in per_eng.get(eng, []):
                    (_, fn, deps, dma, n) = ops[idx]
                    for s, i in deps.items():
                        if isinstance(s, tuple) and s[0] == "dma":
                            k = s[1]
                            lst = dma_idx[k]
                            pos = bisect.bisect_left(lst, idx) - 1
                            j = lst[pos]
                            assert j >= i
                            c = dma_cum[k][j]
                            b = (c - 1) // DL
                            val = (c - b * DL) * 16
                        else:
                            c = ticket[i]
                            b = (c - 1) // SEM_LIMIT
                            val = c - b * SEM_LIMIT
                        sk = (s, b)
                        if waited.get(sk, 0) >= val:
                            continue
                        waited[sk] = val
                        e.wait_ge(sems[sk], val)
                    r = fn(e)
                    if dma is not None:
                        if not isinstance(r, (list, tuple)):
                            r = [r]
                        assert len(r) == n
                        base = dma_cum[dma][idx] - n
                        for q, ins in enumerate(r):
                            c = base + q + 1
                            b = (c - 1) // DL
                            ins.then_inc(sems[(("dma", dma), b)], 16)
                    elif idx in needs_inc:
                        c = ticket[idx]
                        b = (c - 1) // SEM_LIMIT
                        r.then_inc(sems[(eng, b)], 1)

            @block.tensor
            def _(e):
                run_engine(e, "pe")

            @block.scalar
            def _(e):
                run_engine(e, "act")

            @block.vector
            def _(e):
                run_engine(e, "dve")

            @block.gpsimd
            def _(e):
                run_engine(e, "pool")

            @block.sync
            def _(e):
                run_engine(e, "sp")


class Arena:
    def __init__(self, nc, base=16512, limit=229312):
        self.nc = nc
        self.off = base
        self.limit = limit
        self.n = 0

    def alloc(self, shape, dtype):
        nbytes = int(np.prod(shape[1:])) * (4 if dtype == F32 else 2)
        nbytes = (nbytes + 63) // 64 * 64
        assert self.off + nbytes <= self.limit, ("SBUF overflow", self.off, nbytes)
        self.n += 1
        t = self.nc.alloc_sbuf_tensor_at("sb%d" % self.n, list(shape), dtype, offset=self.off)
        self.off += nbytes
        return t

    def mark(self):
        return self.off

    def reset(self, m):
        self.off = m


_UID = [0]


def uid():
    _UID[0] += 1
    return _UID[0]


def subtiles(n):
    out = []
    r = 0
    while r < n:
        out.append((r, min(128, n - r)))
        r += 128
    return out


def build_program(debug=False, phases="WABCF"):
    nc = bass.Bass("TRN2", target_bir_lowering=False)
    pg = Prog()

    def din(name, shape, dt=F32):
        return nc.dram_tensor(name, list(shape), dt, kind="ExternalInput").ap()

    def dscr(name, shape, dt=BF16):
        if debug:
            return nc.dram_tensor(name, list(shape), dt, kind="ExternalOutput").ap()
        return nc.dram_tensor(name, list(shape), dt).ap()

    hseq = din("hseq", [LP, D])
    hext = din("hext", [EXT, D])
    w_in = din("w_in", [D, 3904])
    w_q_up = din("w_q_up", [512, 1536])
    w_kv_up = din("w_kv_up", [256, 2048])
    w_out = din("w_out", [D, D])
    w_gate = din("w_gate", [D, DFF])
    w_up = din("w_up", [D, DFF])
    w_down = din("w_down", [DFF, D])
    ident_in = din("ident", [128, 128])
    gmix_in = din("gmix", [128, D])
    gffn_in = din("gffn", [128, D])
    gvec_in = din("gvec", [128, 64])
    convw_in = din("convw", [128, NFC * 3])
    convb_in = din("convb", [128, NFC])
    metab_in = din("metab", [16, 8 * 64])
    hmask_in = din("hmask", [128, 2])
    cosA = din("cosA", [64, LP])
    sinA = din("sinA", [64, LP])
    cosB = din("cosB", [64, EXT])
    sinB = din("sinB", [64, EXT])
    biasI_in = din("biasI", [128, 8 * 4 * 64])
    biasE_in = din("biasE", [9, 128, 8 * 7 * 64])
    out = nc.dram_tensor("out", [4096, D], F32, kind="ExternalOutput").ap()

    wb_in = dscr("wb_in", [D, 3904])
    wb_q = dscr("wb_q", [512, 1536])
    wb_kv = dscr("wb_kv", [256, 2048])
    wb_out = dscr("wb_out", [D, D])
    wb_gate = dscr("wb_gate", [D, DFF])
    wb_up = dscr("wb_up", [D, DFF])
    wb_down = dscr("wb_down", [DFF, D])
    KN = dscr("KN", [8, 128, LP])
    KR = dscr("KR", [8, 64, LP])
    VS = dscr("VS", [8, 128, NCH, 128])
    QA = dscr("QA", [8, 128, EXT])
    KA = dscr("KA", [8, 128, EXT])
    VA = dscr("VA", [EXT, 1024])
    QN = dscr("QN", [8, 128, EXT])
    QR = dscr("QR", [8, 64, EXT])
    H2 = dscr("H2", [NMIX, D], F32)

    if DBG.get("useA"):
        DBG["ctab"], DBG["stab"] = cosA, sinA
    ar = Arena(nc)
    psum = [nc.alloc_psum_tensor("ps%d" % i, [128, 512], F32) for i in range(6)]
    psbf = [nc.alloc_psum_tensor("pb%d" % i, [128, 8, 128], BF16) for i in range(2)]
    rr = {"ps": 0, "pb": 0}

    def ps_next(pool=(0, 1, 2, 3, 4, 5)):
        i = pool[rr["ps"] % len(pool)]
        rr["ps"] += 1
        return i

    def pb_next():
        i = rr["pb"] % 2
        rr["pb"] += 1
        return i

    def mm(o, lhsT, rhs, start, stop, reads, writes):
        pg.add("pe", lambda e: e.matmul(o, lhsT, rhs, start=start, stop=stop), reads, writes)

    def tr(o, in_, idn, reads, writes):
        pg.add("pe", lambda e: e.transpose(o, in_, idn), reads, writes)

    def act(o, in_, func, reads, writes, **kw):
        pg.add("act", lambda e: e.activation(o, in_, func, **kw), reads, writes)

    def tt(eng, o, a, b, op, reads, writes):
        pg.add(eng, lambda e: e.tensor_tensor(o, a, b, op), reads, writes)

    def ts(eng, o, a, s1, s2, op0, op1, reads, writes):
        if op1 is None:
            pg.add(eng, lambda e: e.tensor_scalar(o, a, s1, None, op0), reads, writes)
        else:
            pg.add(eng, lambda e: e.tensor_scalar(o, a, s1, s2, op0, op1), reads, writes)

    def stt(eng, o, a, s, b, op0, op1, reads, writes):
        pg.add(eng, lambda e: e.scalar_tensor_tensor(o, a, s, b, op0, op1), reads, writes)

    def recip(o, a, reads, writes):
        pg.add("dve", lambda e: e.reciprocal(o, a), reads, writes)

    def cp(eng, o, a, reads, writes):
        if eng == "act":
            pg.add("act", lambda e: e.copy(o, a), reads, writes)
        else:
            pg.add(eng, lambda e: e.tensor_copy(o, a), reads, writes)

    def memset(eng, o, v, writes):
        if eng == "act":
            pg.add(eng, lambda e: e.copy(o, dummy[:, 7:8]), ["dummy"], writes)
        else:
            pg.add(eng, lambda e: e.memset(o, v), (), writes)

    def dma(q, o, i, reads, writes, key):
        pg.add(q, lambda e: e.dma_start(out=o, in_=i), reads, writes, dma=key, n=1)

    def dmas(q, pairs, reads, writes, key):
        pg.add(q, lambda e: [e.dma_start(out=o, in_=i) for (o, i) in pairs], reads, writes, dma=key,
               n=len(pairs))

    dummy = ar.alloc([128, 8], F32)
    pg.add("dve", lambda e: e.memset(dummy[:, :], 0.0), (), ["dummy"])

    def barrier():
        ks = []
        for j, en in enumerate(("act", "dve")):
            k = ("bar", uid())
            ks.append(k)
            memset(en, dummy[:, j:j + 1], 0.0, [k])
        ext = {("dma", k): i for k, i in pg.last_dma.items()}
        for j, en in enumerate(("act", "dve")):
            if en == "act":
                pg.add(en, lambda e: e.copy(dummy[:, 3:4], dummy[:, 7:8]), ks + ["dummy"], [("bar2", uid())], extra=ext)
            else:
                pg.add(en, (lambda jj: (lambda e: e.memset(dummy[:, 3 + jj:4 + jj], 0.0)))(j), ks,
                       [("bar2", uid())], extra=ext)
        for en in ("pe", "sp", "pool"):
            pg.add(en, lambda e: e.nop(), ks, [("bar2", uid())], extra=ext)

    ident_f = ar.alloc([128, 128], F32)
    ident = ar.alloc([128, 128], BF16)
    ones = ar.alloc([128, 128], BF16)
    ones_f = ar.alloc([128, 128], F32)
    gvec = ar.alloc([128, 64], F32)
    convw = ar.alloc([128, NFC * 3], F32)
    convb = ar.alloc([128, NFC], F32)
    metab = ar.alloc([16, 512], F32)
    hmask = ar.alloc([128, 2], F32)
    epsc = ar.alloc([128, 8], F32)
    dma("sp", ident_f[:], ident_in[:, :], (), ["ident_f"], "c0")
    dma("sp", gvec[:], gvec_in[:, :], (), ["gvec"], "c1")
    dma("sp", convw[:], convw_in[:, :], (), ["convw"], "c2")
    dma("sp", convb[:], convb_in[:, :], (), ["convb"], "c3")
    dma("sp", metab[:], metab_in[:, :], (), ["metab"], "c4")
    dma("sp", hmask[:], hmask_in[:, :], (), ["hmask"], "c5")
    cp("dve", ident[:], ident_f[:], ["ident_f"], ["ident"])
    memset("dve", ones[:], 1.0, ["ones"])
    memset("dve", ones_f[:], 1.0, ["ones_f"])
    memset("dve", epsc[:, 0:1], EPS, ["epsc"])
    memset("dve", epsc[:, 1:2], EPS * 192.0, ["epsc"])
    memset("dve", epsc[:, 2:3], EPS * 128.0, ["epsc"])
    GC_NAQ, GC_NAK, GC_QN, GC_QR, GC_QRR, GC_KN, GC_KR, GC_KRR = 0, 1, 2, 3, 4, 5, 6, 7
    GC_CQ, GC_CKV, GC_NAO, GC_MLO = 8, 12, 16, 24

    def wconv(dst, src, rows, key):
        for r in range(0, rows, 256):
            rr_ = min(256, rows - r)
            dma("pool", dst[r:r + rr_, :], src[r:r + rr_, :], (), ["%s_%d" % (key, r)],
                "w_%s_%d" % (key, (r // 256) % 4))

    def wkeys(key, r0, r1):
        return ["%s_%d" % (key, r) for r in range((r0 // 256) * 256, r1, 256)]

    wconv(wb_in, w_in, D, "wb_in")
    wconv(wb_kv, w_kv_up, 256, "wb_kv")
    wconv(wb_q, w_q_up, 512, "wb_q")
    wconv(wb_out, w_out, D, "wb_out")
    wconv(wb_gate, w_gate, D, "wb_gate")
    wconv(wb_up, w_up, D, "wb_up")
    wconv(wb_down, w_down, DFF, "wb_down")

    m_persist = ar.mark()

    def alloc_norm_bufs():
        nb = {}
        nb["xin"] = [ar.alloc([128, D], F32) for _ in range(2)]
        nb["xs"] = [ar.alloc([128, D], BF16) for _ in range(2)]
        nb["junk"] = ar.alloc([128, D], BF16)
        nb["ssq"] = [ar.alloc([128, 1], F32) for _ in range(2)]
        nb["rs"] = [ar.alloc([128, 1], F32) for _ in range(2)]
        nb["rstd"] = [ar.alloc([128, 1], F32) for _ in range(2)]
        nb["cnt"] = 0
        return nb

    def norm_transpose(src_fn, n, gfull, hnT, col0, nb, tag, src_reads=()):
        for (r0, ns) in subtiles(n):
            sl = nb["cnt"] % 2
            nb["cnt"] += 1
            xi = nb["xin"][sl]
            xs = nb["xs"][sl]
            ssq, rs, rstd = nb["ssq"][sl], nb["rs"][sl], nb["rstd"][sl]
            dma("sp", xi[0:ns, :], src_fn(r0, ns), list(src_reads), [(tag, "xin", sl)], "%s_xin%d" % (tag, sl))
            memset("dve", ssq[0:ns, :], 0.0, [(tag, "ssq", sl)])
            act(nb["junk"][0:ns, :], xi[0:ns, :], AF.Square, [(tag, "xin", sl), (tag, "ssq", sl)],
                [(tag, "junk"), (tag, "ssq", sl)], accum_out=ssq[0:ns, :])
            act(rs[0:ns, :], ssq[0:ns, :], AF.Sqrt, [(tag, "ssq", sl), "epsc"], [(tag, "rs", sl)],
                scale=1.0 / D, bias=epsc[0:ns, 0:1])
            recip(rstd[0:ns, :], rs[0:ns, :], [(tag, "rs", sl)], [(tag, "rstd", sl)])
            stt("dve", xs[0:ns, :], xi[0:ns, :], rstd[0:ns, 0:1], gfull[0:ns, :], ALU.mult, ALU.mult,
                [(tag, "xin", sl), (tag, "rstd", sl), "gfull"], [(tag, "xs", sl)])
            for half in range(2):
                b = pb_next()
                pt = psbf[b]
                for j in range(8):
                    kc = half * 8 + j
                    tr(pt[:, j, 0:ns], xs[0:ns, kc * 128:(kc + 1) * 128], ident[0:ns, 0:ns],
                       [(tag, "xs", sl), "ident"], [("pb", b)])
                eng = "act" if half == 0 else "dve"
                cp(eng, hnT[:, half * 8:(half + 1) * 8, col0 + r0:col0 + r0 + ns], pt[:, :, 0:ns], [("pb", b)],
                   [("hnT", half)])

    HN = [("hnT", 0), ("hnT", 1)]

    def rstd_bc(sum_ap, b_sum, n, npart, scale, bias_col, rs_t, rstd_t, tag):
        act(rs_t[0:npart, 0:n], sum_ap, AF.Sqrt, [("ps", b_sum), "epsc"], [("T", "rsb")], scale=scale,
            bias=epsc[0:npart, bias_col:bias_col + 1])
        recip(rstd_t[0:npart, 0:n], rs_t[0:npart, 0:n], [("T", "rsb")], [("T", "rstdb")])

    def qk_norm_rope(tag, n, nope_b, kN, kR, sqA, t3, gcol_n, bias_col, scale, tmp, outN, outR):
        sqn = tmp["sqn"]
        act(sqn[:, 0:n], psum[nope_b][:, 0:n], AF.Square, [("ps", nope_b)], [("T", "sqn")])
        bs = ps_next()
        mm(psum[bs][:, 0:n], ones[:, :], sqn[:, 0:n], True, False, ["ones", ("T", "sqn")], [("ps", bs)])
        mm(psum[bs][:, 0:n], ones[0:64, :], sqA[0:64, 0:n], False, True, ["ones", ("T", "sqA")], [("ps", bs)])
        rstd_bc(psum[bs][:, 0:n], bs, n, 128, scale, bias_col, tmp["rs"], tmp["rstd"], tag)
        stt("dve", outN[:, 0:n], psum[nope_b][:, 0:n], gvec[:, gcol_n:gcol_n + 1], tmp["rstd"][:, 0:n],
            ALU.mult, ALU.mult, [("ps", nope_b), "gvec", ("T", "rstdb")], [kN])
        tt("dve", outR[0:64, 0:n], t3[0:64, 0:n], tmp["rstd"][0:64, 0:n], ALU.mult,
           [("T", "t3"), ("T", "rstdb")], [kR])

    def rope_tables(tag, n, ctab, stab, pos0, gc_r, gc_rr, tmp):
        c2, s2 = tmp["c2"], tmp["s2"]
        if DBG.get("nodma"):
            memset("dve", c2[0:64, 0:n], 1.0, [("T", "c2")])
            memset("dve", s2[0:64, 0:n], 1.0, [("T", "s2")])
        else:
            dma("sp", c2[0:64, 0:n], ctab[:, pos0:pos0 + n], (), [("T", "c2")], tag + "_c2")
            dma("sp", s2[0:64, 0:n], stab[:, pos0:pos0 + n], (), [("T", "s2")], tag + "_s2")
        ts("dve", c2[0:64, 0:n], c2[0:64, 0:n], gvec[0:64, gc_r:gc_r + 1], None, ALU.mult, None,
           [("T", "c2"), "gvec"], [("T", "c2")])
        ts("dve", s2[0:64, 0:n], s2[0:64, 0:n], gvec[0:64, gc_rr:gc_rr + 1], None, ALU.mult, None,
           [("T", "s2"), "gvec"], [("T", "s2")])

    def rope_apply(n, A_b, B_b, tmp):
        c2, s2 = tmp["c2"], tmp["s2"]
        act(tmp["sqA"][0:64, 0:n], psum[A_b][0:64, 0:n], AF.Square, [("ps", A_b)], [("T", "sqA")])
        cp("act", tmp["t1"][0:64, 0:n], psum[A_b][0:64, 0:n], [("ps", A_b)], [("T", "t1")])
        cp("act", tmp["t2"][0:64, 0:n], psum[B_b][0:64, 0:n], [("ps", B_b)], [("T", "t2")])
        tt("dve", tmp["t1"][0:64, 0:n], tmp["t1"][0:64, 0:n], c2[0:64, 0:n], ALU.mult, [("T", "t1"), ("T", "c2")],
           [("T", "t1")])
        tt("dve", tmp["t2"][0:64, 0:n], tmp["t2"][0:64, 0:n], s2[0:64, 0:n], ALU.mult, [("T", "t2"), ("T", "s2")],
           [("T", "t2")])
        tt("dve", tmp["t3"][0:64, 0:n], tmp["t1"][0:64, 0:n], tmp["t2"][0:64, 0:n], ALU.add,
           [("T", "t1"), ("T", "t2")], [("T", "t3")])

    def alloc_qk_tmp():
        t = {}
        t["sqn"] = ar.alloc([128, 512], BF16)
        t["sqA"] = ar.alloc([64, 512], BF16)
        t["rs"] = ar.alloc([128, 512], F32)
        t["rstd"] = ar.alloc([128, 512], F32)
        t["c2"] = ar.alloc([64, 512], F32)
        t["s2"] = ar.alloc([64, 512], F32)
        t["t1"] = ar.alloc([64, 512], F32)
        t["t2"] = ar.alloc([64, 512], F32)
        t["t3"] = ar.alloc([64, 512], F32)
        return t

    if "A" in phases:
        gfull = ar.alloc([128, D], F32)
        dma("sp", gfull[:], gmix_in[:, :], (), ["gfull"], "gfull")
        nb = alloc_norm_bufs()
        hnT = ar.alloc([128, 16, 512], BF16)
        wlat = ar.alloc([128, 16, 384], BF16)
        wkv = ar.alloc([128, 2, 8, 256], BF16)
        ckvn = ar.alloc([128, 2, 512], BF16)
        sqc = ar.alloc([128, 2, 512], BF16)
        qt = alloc_qk_tmp()
        knst = [ar.alloc([128, 512], BF16) for _ in range(2)]
        krst = [ar.alloc([64, 512], BF16) for _ in range(2)]
        vst = [ar.alloc([128, 8, 128], BF16) for _ in range(2)]
        dma("sp", wlat[:, :, 0:320], wb_in[:, 3584:3904].rearrange("(kc p) c -> p kc c", p=128),
            wkeys("wb_in", 0, D), ["wlat"], "wlat")
        act(wlat[:, :, 320:352], wlat[:, :, 288:320], AF.Copy, ["wlat"], ["wlat_r"], scale=-1.0)
        cp("dve", wlat[:, :, 352:384], wlat[:, :, 256:288], ["wlat"], ["wlat_r"])
        dma("sp", wkv[:].rearrange("p j h d -> p j (h d)"), wb_kv.rearrange("(j p) c -> p j c", p=128),
            wkeys("wb_kv", 0, 256), ["wkv"], "wkv")
        cntA = 0
        for t0 in range(0, L, 512):
            n = min(512, L - t0)
            tag = "A"
            norm_transpose(lambda r0, ns: hseq[t0 + r0:t0 + r0 + ns, :], n, gfull, hnT, 0, nb, "A")
            lat_b = []
            for (c0, c1) in ((0, 128), (128, 256), (256, 320), (320, 384)):
                b = ps_next()
                lat_b.append(b)
                m = c1 - c0
                for kc in range(16):
                    mm(psum[b][0:m, 0:n], wlat[:, kc, c0:c1], hnT[:, kc, 0:n], kc == 0, kc == 15,
                       ["wlat", "wlat_r", HN[kc // 8]], [("ps", b)])
            for j in range(2):
                act(sqc[:, j, 0:n], psum[lat_b[j]][:, 0:n], AF.Square, [("ps", lat_b[j])], [("A", "sqc")])
            bs = ps_next()
            for j in range(2):
                mm(psum[bs][:, 0:n], ones[:, :], sqc[:, j, 0:n], j == 0, j == 1, ["ones", ("A", "sqc")], [("ps", bs)])
            rstd_bc(psum[bs][:, 0:n], bs, n, 128, 1.0 / 256, 0, qt["rs"], qt["rstd"], "Ac")
            for j in range(2):
                stt("dve", ckvn[:, j, 0:n], psum[lat_b[j]][:, 0:n], gvec[:, GC_CKV + j:GC_CKV + j + 1],
                    qt["rstd"][:, 0:n], ALU.mult, ALU.mult, [("ps", lat_b[j]), "gvec", ("T", "rstdb")],
                    [("A", "ckvn")])
            rope_tables("A", n, cosA, sinA, t0, GC_KR, GC_KRR, qt)
            rope_apply(n, lat_b[2], lat_b[3], qt)
            for h in range(8):
                b = ps_next()
                for j in range(2):
                    mm(psum[b][:, 0:n], wkv[:, j, h, 0:128], ckvn[:, j, 0:n], j == 0, j == 1,
                       ["wkv", ("A", "ckvn")], [("ps", b)])
                sl = cntA % 2
                cntA += 1
                qk_norm_rope("A", n, b, ("A", "knst", sl), ("A", "krst", sl), qt["sqA"], qt["t3"], GC_KN, 0, 1.0 / 192, qt,
                             knst[sl], krst[sl])
                dma("pool", KN[h, :, t0:t0 + n], knst[sl][:, 0:n], [("A", "knst", sl)], [("KN", h, t0 // 512)],
                    "A_kn%d" % sl)
                dma("pool", KR[h, :, t0:t0 + n], krst[sl][0:64, 0:n], [("A", "krst", sl)], [("KR", h, t0 // 512)],
                    "A_kr%d" % sl)
            for (r0, ns) in subtiles(n):
                sl = cntA % 2
                cntA += 1
                ch = (t0 + r0) // 128
                for half in range(2):
                    b = ps_next()
                    for j in range(2):
                        mm(psum[b][0:ns, :].rearrange("p (h d) -> p h d", d=128), ckvn[:, j, r0:r0 + ns],
                           wkv[:, j, half * 4:half * 4 + 4, 128:256], j == 0, j == 1, ["wkv", ("A", "ckvn")],
                           [("ps", b)])
                    cp("act" if half == 0 else "dve", vst[sl][0:ns, half * 4:half * 4 + 4, :],
                       psum[b][0:ns, :].rearrange("p (h d) -> p h d", d=128), [("ps", b)], [("A", "vst", sl, half)])
                dma("pool", VS[:, 0:ns, ch, :].rearrange("h p d -> p h d"), vst[sl][0:ns, :, :],
                    [("A", "vst", sl, 0), ("A", "vst", sl, 1)], [("VS", ch)], "A_vs%d" % sl)
        barrier()
        ar.reset(m_persist)

    if "B" in phases:
        gfull = ar.alloc([128, D], F32)
        dma("sp", gfull[:], gmix_in[:, :], (), ["gfull"], "gfullB")
        qt = alloc_qk_tmp()
        nb = alloc_norm_bufs()
        hnT = ar.alloc([128, 16, 512], BF16)
        wsl = [ar.alloc([128, 16, 512], BF16) for _ in range(3)]
        wq = ar.alloc([128, 4, 8, 192], BF16)
        wqr = ar.alloc([128, 4, 8, 64], BF16)
        cqn = ar.alloc([128, 4, 512], BF16)
        sqc = ar.alloc([128, 4, 512], BF16)
        ost = [ar.alloc([128, 512], BF16) for _ in range(2)]
        orst = [ar.alloc([64, 512], BF16) for _ in range(2)]
        vast = [ar.alloc([128, 512], BF16) for _ in range(2)]
        dma("sp", wq[:].rearrange("p j h d -> p j (h d)"), wb_q.rearrange("(j p) c -> p j c", p=128),
            wkeys("wb_q", 0, 512), ["wq"], "wq")
        for j in range(4):
            act(wqr[:, j, :, 0:32], wq[:, j, :, 160:192], AF.Copy, ["wq"], ["wqr"], scale=-1.0)
            cp("dve", wqr[:, j, :, 32:64], wq[:, j, :, 128:160], ["wq"], ["wqr"])
        cntB = 0
        wcnt = 0
        for t0 in DBG.get("btiles", range(0, EXT, 512)):
            n = min(512, EXT - t0)
            norm_transpose(lambda r0, ns: hext[t0 + r0:t0 + r0 + ns, :], n, gfull, hnT, 0, nb, "B")
            for blk in DBG.get("bblks", range(7)):
                ws = wcnt % 3
                wcnt += 1
                dma("sp", wsl[ws][:], wb_in[:, blk * 512:(blk + 1) * 512].rearrange("(kc p) c -> p kc c", p=128),
                    wkeys("wb_in", 0, D), [("wsl", ws)], "B_w%d" % ws)
                if blk < 4:
                    for hh in range(4):
                        h = (blk % 2) * 4 + hh
                        b = ps_next()
                        for kc in range(16):
                            mm(psum[b][:, 0:n], wsl[ws][:, kc, hh * 128:(hh + 1) * 128], hnT[:, kc, 0:n], kc == 0,
                               kc == 15, [("wsl", ws), HN[kc // 8]], [("ps", b)])
                        act(qt["sqn"][:, 0:n], psum[b][:, 0:n], AF.Square, [("ps", b)], [("T", "sqn")])
                        bs = ps_next()
                        mm(psum[bs][:, 0:n], ones[:, :], qt["sqn"][:, 0:n], True, True, ["ones", ("T", "sqn")],
                           [("ps", bs)])
                        if blk < 2:
                            rstd_bc(psum[bs][:, 0:n], bs, n, 128, 1.0, 2, qt["rs"], qt["rstd"], "Bq")
                        else:
                            rstd_bc(psum[bs][:, 0:n], bs, n, 128, 1.0 / 128, 0, qt["rs"], qt["rstd"], "Bq")
                        sl = cntB % 2
                        cntB += 1
                        gc = GC_NAQ if blk < 2 else GC_NAK
                        stt("dve", ost[sl][:, 0:n], psum[b][:, 0:n], gvec[:, gc:gc + 1], qt["rstd"][:, 0:n],
                            ALU.mult, ALU.mult, [("ps", b), "gvec", ("T", "rstdb")], [("B", "ost", sl)])
                        dst = QA if blk < 2 else KA
                        dma("pool", dst[h, :, t0:t0 + n], ost[sl][:, 0:n], [("B", "ost", sl)],
                            [("QA" if blk < 2 else "KA", h, t0 // 512)], "B_o%d" % sl)
                elif blk < 6:
                    vb = blk - 4
                    for (r0, ns) in subtiles(n):
                        b = ps_next()
                        for kc in range(16):
                            mm(psum[b][0:ns, :], hnT[:, kc, r0:r0 + ns], wsl[ws][:, kc, :], kc == 0, kc == 15,
                               [("wsl", ws), HN[kc // 8]], [("ps", b)])
                        sl = cntB % 2
                        cntB += 1
                        cp("act", vast[sl][0:ns, :], psum[b][0:ns, :], [("ps", b)], [("B", "vast", sl)])
                        dma("pool", VA[t0 + r0:t0 + r0 + ns, vb * 512:(vb + 1) * 512], vast[sl][0:ns, :],
                            [("B", "vast", sl)], [("VA", (t0 + r0) // 128, vb)], "B_va%d" % sl)
                else:
                    cb = []
                    for j in range(4):
                        b = ps_next()
                        cb.append(b)
                        for kc in range(16):
                            mm(psum[b][:, 0:n], wsl[ws][:, kc, j * 128:(j + 1) * 128], hnT[:, kc, 0:n], kc == 0,
                               kc == 15, [("wsl", ws), HN[kc // 8]], [("ps", b)])
                        act(sqc[:, j, 0:n], psum[b][:, 0:n], AF.Square, [("ps", b)], [("B", "sqc")])
                    bs = ps_next()
                    for j in range(4):
                        mm(psum[bs][:, 0:n], ones[:, :], sqc[:, j, 0:n], j == 0, j == 3, ["ones", ("B", "sqc")],
                           [("ps", bs)])
                    rstd_bc(psum[bs][:, 0:n], bs, n, 128, 1.0 / 512, 0, qt["rs"], qt["rstd"], "Bc")
                    for j in range(4):
                        stt("dve", cqn[:, j, 0:n], psum[cb[j]][:, 0:n], gvec[:, GC_CQ + j:GC_CQ + j + 1],
                            qt["rstd"][:, 0:n], ALU.mult, ALU.mult, [("ps", cb[j]), "gvec", ("T", "rstdb")],
                            [("B", "cqn")])
            nq = min(n, max(0, EXTG - t0))
            if nq > 0 and DBG.get("bup", True):
                rope_tables("Bh", nq, cosB, sinB, t0, GC_QR, GC_QRR, qt)
                for h in DBG.get("bheads", range(8)):
                    bn, bA, bB = ps_next(), ps_next(), ps_next()
                    for j in range(4):
                        mm(psum[bn][:, 0:nq], wq[:, j, h, 0:128], cqn[:, j, 0:nq], j == 0, j == 3,
                           ["wq", ("B", "cqn")], [("ps", bn)])
                    for j in range(4):
                        mm(psum[bA][0:64, 0:nq], wq[:, j, h, 128:192], cqn[:, j, 0:nq], j == 0, j == 3,
                           ["wq", ("B", "cqn")], [("ps", bA)])
                    for j in range(4):
                        mm(psum[bB][0:64, 0:nq], wqr[:, j, h, :], cqn[:, j, 0:nq], j == 0, j == 3,
                           ["wqr", ("B", "cqn")], [("ps", bB)])
                    if DBG.get("bstage", 3) == 1:
                        for bb_ in (bn, bA, bB):
                            act(qt["sqn"][0:64, 0:nq], psum[bb_][0:64, 0:nq], AF.Square, [("ps", bb_)], [("T", "sqn")])
                        continue
                    rope_apply(nq, bA, bB, qt)
                    if DBG.get("bstage", 3) == 2:
                        act(qt["sqn"][:, 0:nq], psum[bn][:, 0:nq], AF.Square, [("ps", bn)], [("T", "sqn")])
                        continue
                    sl = cntB % 2
                    cntB += 1
                    qk_norm_rope("Bh", nq, bn, ("B", "ost", sl), ("B", "orst", sl), qt["sqA"], qt["t3"], GC_QN, 1, 1.0, qt,
                                 ost[sl], orst[sl])
                    dma("pool", QN[h, :, t0:t0 + nq], ost[sl][:, 0:nq], [("B", "ost", sl)], [("QN", h, t0 // 512)],
                        "B_o%d" % sl)
                    dma("pool", QR[h, :, t0:t0 + nq], orst[sl][0:64, 0:nq], [("B", "orst", sl)], [("QR", h, t0 // 512)],
                        "B_or%d" % sl)
        barrier()
        ar.reset(m_persist)

    if "C" in phases:
        OT = ar.alloc([128, 16, 512], F32)
        mixT = ar.alloc([128, 16, 512], BF16)
        kam = ar.alloc([128, 8, 16], BF16)
        vam = ar.alloc([16, 1024], BF16)
        sqo = ar.alloc([128, 512], BF16)
        rsO = ar.alloc([128, 512], F32)
        rstdO = ar.alloc([128, 512], F32)
        dma("sp", kam[:], KA[:, :, EXTG:EXT].rearrange("h p k -> p h k"), [("KA", h, 9) for h in range(8)],
            ["kam"], "kam")
        dma("sp", vam[:], VA[EXTG:EXT, :], [("VA", 37, 0), ("VA", 37, 1)], ["vam"], "vam")
        m_c = ar.mark()
        wcnt = 0
        for m in range(9):
            e0 = MIX0 + 512 * m
            n = min(512, MIX0 + NMIX - e0)
            nrows = n // 64
            t5 = e0 // 512
            ar.reset(m_c)
            qa = ar.alloc([128, 8, 512], BF16)
            kw = [ar.alloc([128, 8, 896], BF16) for _ in range(2)]
            vw = [ar.alloc([128, 7, 1024], BF16) for _ in range(2)]
            bI = ar.alloc([128, 8, 4 * 64], F32)
            bE = ar.alloc([128, 8, 7 * 64], F32)
            sS = [ar.alloc([128, 448], F32) for _ in range(2)]
            pT = [ar.alloc([128, 448], BF16) for _ in range(2)]
            pM = ar.alloc([16, 512], BF16)
            sM = ar.alloc([16, 512], F32)
            rden = ar.alloc([128, 4, 64], F32)
            qkeys = []
            for h in range(8):
                for tt_ in range(e0 // 512, (e0 + n - 1) // 512 + 1):
                    qkeys.append(("QA", h, tt_))
            dma("sp", qa[:, :, 0:n], QA[:, :, e0:e0 + n].rearrange("h p q -> p h q"), qkeys, ["qa"], "C_qa")
            dma("sp", bI[:].rearrange("p h k -> p (h k)"), biasI_in[:, :], (), ["bI"], "C_bI")
            for r in range(nrows):
                rl = -1 + 8 * m + r
                re = rl + 5
                if rl <= 3:
                    k0, nch, ei = 0, 7, rl + 1
                elif rl >= 61:
                    k0, nch, ei = 60 * 64, 7, 5 + (rl - 61)
                else:
                    k0, nch, ei = (re - 4) * 64, 4, None
                W = nch * 128
                sl = r % 2
                kkeys = [("KA", h, t) for h in range(8) for t in range(k0 // 512, (k0 + W - 1) // 512 + 1)]
                vkeys = [("VA", c, vb) for c in range(k0 // 128, (k0 + W) // 128) for vb in range(2)]
                dma("sp", kw[sl][:, :, 0:W], KA[:, :, k0:k0 + W].rearrange("h p k -> p h k"), kkeys, [("kw", sl)],
                    "C_kw%d" % sl)
                dma("pool", vw[sl][:, 0:nch, :], VA[k0:k0 + W, :].rearrange("(c p) d -> p c d", p=128), vkeys,
                    [("vw", sl)], "C_vw%d" % sl)
                if ei is not None:
                    dma("sp", bE[:].rearrange("p h k -> p (h k)"), biasE_in[ei, :, :], (), ["bE"], "C_bE")
                q0 = r * 64
                bm = ps_next()
                for h in range(8):
                    mm(psum[bm][0:16, h * 64:(h + 1) * 64], kam[:, h, :], qa[:, h, q0:q0 + 64], True, True,
                       ["kam", "qa"], [("ps", bm)])
                tt("dve", sM[:, :], psum[bm][0:16, :], metab[:, :], ALU.add, [("ps", bm), "metab"], ["sM"])
                act(pM[:, :], sM[:, :], AF.Exp, ["sM"], ["pM"])
                for hg in range(2):
                    bo = ps_next()
                    for hh in range(4):
                        h = hg * 4 + hh
                        b = ps_next()
                        for c in range(nch):
                            mm(psum[b][:, c * 64:(c + 1) * 64], kw[sl][:, h, c * 128:(c + 1) * 128],
                               qa[:, h, q0:q0 + 64], True, True, [("kw", sl), "qa"], [("ps", b)])
                        s2 = h % 2
                        btab = bI[:, h, :] if ei is None else bE[:, h, :]
                        tt("dve", sS[s2][:, 0:nch * 64], psum[b][:, 0:nch * 64], btab,
                           ALU.add, [("ps", b), "bI", "bE"], [("sS", s2)])
                        act(pT[s2][:, 0:nch * 64], sS[s2][:, 0:nch * 64], AF.Exp, [("sS", s2)], [("pT", s2)])
                        o_ap = psum[bo][:, (hh * 2) * 64:(hh * 2 + 1) * 64]
                        d_ap = psum[bo][:, (hh * 2 + 1) * 64:(hh * 2 + 2) * 64]
                        for c in range(nch):
                            mm(o_ap, vw[sl][:, c, h * 128:(h + 1) * 128], pT[s2][:, c * 64:(c + 1) * 64], c == 0,
                               False, [("vw", sl), ("pT", s2)], [("ps", bo)])
                        mm(o_ap, vam[:, h * 128:(h + 1) * 128], pM[:, h * 64:(h + 1) * 64], False, True,
                           ["vam", "pM"], [("ps", bo)])
                        for c in range(nch):
                            mm(d_ap, ones[:, :], pT[s2][:, c * 64:(c + 1) * 64], c == 0, False,
                               ["ones", ("pT", s2)], [("ps", bo)])
                        mm(d_ap, ones[0:16, :], pM[:, h * 64:(h + 1) * 64], False, True, ["ones", "pM"],
                           [("ps", bo)])
                    pv = psum[bo][:, :].rearrange("p (h t q) -> p h t q", t=2, q=64)
                    recip(rden[:, :, :], pv[:, :, 1, :], [("ps", bo)], ["rden"])
                    tt("dve", OT[:, hg * 4:hg * 4 + 4, q0:q0 + 64], pv[:, :, 0, :], rden[:, :, :], ALU.mult,
                       [("ps", bo), "rden"], [("OT", hg)])
            barrier()
            ar.reset(m_c)
            qn = ar.alloc([128, 8, 512], BF16)
            qr = ar.alloc([64, 8, 512], BF16)
            knb = [ar.alloc([128, 2048], BF16) for _ in range(2)]
            krb = [ar.alloc([64, 2048], BF16) for _ in range(2)]
            vb_ = [ar.alloc([128, 16, 128], BF16) for _ in range(2)]
            pT = [ar.alloc([128, 512], BF16) for _ in range(3)]
            dacc2 = [ar.alloc([128, 512], F32) for _ in range(2)]
            rden = ar.alloc([128, 512], F32)
            tks = list(range(e0 // 512, min((e0 + n - 1) // 512, 9) + 1))
            dma("sp", qn[:, :, 0:n], QN[:, :, e0:e0 + n].rearrange("h p q -> p h q"),
                [("QN", h, t) for h in range(8) for t in tks], ["qn"], "C_qn")
            dma("sp", qr[:, :, 0:n], QR[:, :, e0:e0 + n].rearrange("h p q -> p h q"),
                [("QR", h, t) for h in range(8) for t in tks], ["qr"], "C_qr")
            kvc = 0
            pc = 0
            SB = (0, 1, 2)
            for h in range(8):
                memset("dve", dacc2[0][:, :], 0.0, [("dacc", 0)])
                memset("dve", dacc2[1][:, :], 0.0, [("dacc", 1)])
                accb = 3 if h % 2 == 0 else 5
                chunks = []
                for blk in range(9):
                    kk0 = blk * 2048
                    nk = min(2048, L - kk0)
                    nchk = (nk + 127) // 128
                    for c in range(nchk):
                        chunks.append((blk, c, kk0, nk, nchk))
                state = {}

                def qk_stage(i, h=h, chunks=chunks, state=state):
                    nonlocal kvc, pc
                    blk, c, kk0, nk, nchk = chunks[i]
                    if c == 0:
                        sl = kvc % 2
                        kvc += 1
                        state[blk] = sl
                        tl = list(range(kk0 // 512, (kk0 + nk - 1) // 512 + 1))
                        dma("sp", knb[sl][:, 0:nk], KN[h, :, kk0:kk0 + nk], [("KN", h, t) for t in tl],
                            [("knb", sl)], "C_kn%d" % sl)
                        dma("sp", krb[sl][0:64, 0:nk], KR[h, :, kk0:kk0 + nk], [("KR", h, t) for t in tl],
                            [("krb", sl)], "C_kr%d" % sl)
                        c0 = kk0 // 128
                        if nk >= 128:
                            dma("pool", vb_[sl][:, 0:nchk, :], VS[h, :, c0:c0 + nchk, :],
                                [("VS", cc) for cc in range(c0, c0 + nchk)], [("vb", sl)], "C_vb%d" % sl)
                        else:
                            dma("pool", vb_[sl][0:nk, 0:1, :], VS[h, 0:nk, c0:c0 + 1, :], [("VS", c0)],
                                [("vb", sl)], "C_vb%d" % sl)
                    sl = state[blk]
                    nkc = min(128, nk - c * 128)
                    ps_ = pc % 3
                    pc += 1
                    b = SB[ps_]
                    state[("b", i)] = ps_
                    mm(psum[b][0:nkc, 0:n], knb[sl][:, c * 128:c * 128 + nkc], qn[:, h, 0:n], True, False,
                       [("knb", sl), "qn"], [("ps", b)])
                    mm(psum[b][0:nkc, 0:n], krb[sl][0:64, c * 128:c * 128 + nkc], qr[0:64, h, 0:n], False, True,
                       [("krb", sl), "qr"], [("ps", b)])

                def pv_stage(i, h=h, chunks=chunks, state=state, accb=accb):
                    blk, c, kk0, nk, nchk = chunks[i]
                    sl = state[blk]
                    nkc = min(128, nk - c * 128)
                    ps_ = state[("b", i)]
                    b = SB[ps_]
                    act(pT[ps_][0:nkc, 0:n], psum[b][0:nkc, 0:n], AF.Exp, [("ps", b)], [("pTm", ps_)])
                    mm(psum[accb][:, 0:n], vb_[sl][0:nkc, c, :], pT[ps_][0:nkc, 0:n], i == 0, i == len(chunks) - 1,
                       [("vb", sl), ("pTm", ps_)], [("ps", accb)])
                    dd = i % 2
                    tt("dve", dacc2[dd][0:nkc, 0:n], dacc2[dd][0:nkc, 0:n], pT[ps_][0:nkc, 0:n], ALU.add,
                       [("pTm", ps_), ("dacc", dd)], [("dacc", dd)])

                qk_stage(0)
                for i in range(len(chunks)):
                    if i + 1 < len(chunks):
                        qk_stage(i + 1)
                    pv_stage(i)
                mm(psum[4][:, 0:n], ones_f[:, :], dacc2[0][:, 0:n], True, False, ["ones_f", ("dacc", 0)], [("ps", 4)])
                mm(psum[4][:, 0:n], ones_f[:, :], dacc2[1][:, 0:n], False, True, ["ones_f", ("dacc", 1)], [("ps", 4)])
                recip(rden[:, 0:n], psum[4][:, 0:n], [("ps", 4)], ["rdenm"])
                tt("dve", OT[:, 8 + h, 0:n], psum[accb][:, 0:n], rden[:, 0:n], ALU.mult, [("ps", accb), "rdenm"],
                   [("OT", 2)])
            for g in range(2):
                bs = ps_next((4, 5))
                for h in range(8):
                    act(sqo[:, 0:n], OT[:, g * 8 + h, 0:n], AF.Square, [("OT", 0), ("OT", 1), ("OT", 2)], ["sqo"])
                    mm(psum[bs][:, 0:n], ones[:, :], sqo[:, 0:n], h == 0, h == 7, ["ones", "sqo"], [("ps", bs)])
                rstd_bc(psum[bs][:, 0:n], bs, n, 128, 1.0 / 1024, 0, rsO, rstdO, "Cg")
                gc = GC_NAO if g == 0 else GC_MLO
                for h in range(8):
                    stt("dve", mixT[:, g * 8 + h, 0:n], OT[:, g * 8 + h, 0:n],
                        gvec[:, gc + h:gc + h + 1], rstdO[:, 0:n], ALU.mult, ALU.mult,
                        [("OT", 0), ("OT", 1), ("OT", 2), "gvec", ("T", "rstdb")], ["mixT"])
            barrier()
            ar.reset(m_c)
            wfull = ar.alloc([128, 16, D], BF16)
            xsl = [ar.alloc([128, D], F32) for _ in range(2)]
            for cb in range(4):
                dma("sp" if cb % 2 == 0 else "pool", wfull[:, :, cb * 512:(cb + 1) * 512],
                    wb_out[:, cb * 512:(cb + 1) * 512].rearrange("(kc p) c -> p kc c", p=128),
                    wkeys("wb_out", 0, D), [("wfull", cb)], "C_wf%d" % cb)
            for si, (r0, ns) in enumerate(subtiles(n)):
                sl = si % 2
                dma("sp", xsl[sl][0:ns, :], hext[e0 + r0:e0 + r0 + ns, :], (), [("xsl", sl)], "C_x%d" % sl)
                for cb in range(4):
                    b = ps_next((0, 1, 2, 5))
                    for kc in range(16):
                        mm(psum[b][0:ns, :], mixT[:, kc, r0:r0 + ns], wfull[:, kc, cb * 512:(cb + 1) * 512], kc == 0,
                           kc == 15, ["mixT", ("wfull", cb)], [("ps", b)])
                    tt("dve", xsl[sl][0:ns, cb * 512:(cb + 1) * 512], psum[b][0:ns, :],
                       xsl[sl][0:ns, cb * 512:(cb + 1) * 512], ALU.add, [("ps", b), ("xsl", sl)], [("xsl", sl)])
                dma("pool", H2[e0 - MIX0 + r0:e0 - MIX0 + r0 + ns, :], xsl[sl][0:ns, :], [("xsl", sl)],
                    [("H2", m, si)], "C_h2%d" % sl)
            barrier()
        ar.reset(m_persist)

    if "F" in phases:
        gfull = ar.alloc([128, D], F32)
        dma("sp", gfull[:], gffn_in[:, :], (), ["gfull"], "gfullF")
        nb = alloc_norm_bufs()
        hnT = ar.alloc([128, 16, 514], BF16)
        aT = ar.alloc([128, NFC, 512], BF16)
        wg = [ar.alloc([128, 16, 256], BF16) for _ in range(2)]
        wu = [ar.alloc([128, 16, 256], BF16) for _ in range(2)]
        wd = [ar.alloc([128, 11, 512], BF16) for _ in range(2)]
        h2r = ar.alloc([128, 4, D], F32)
        gsb = [ar.alloc([128, 514], F32) for _ in range(2)]
        tcv = [ar.alloc([128, 512], F32) for _ in range(2)]
        sg = [ar.alloc([128, 512], F32) for _ in range(2)]
        wc = 0
        wdc = 0
        for f in range(8):
            hr0 = 63 + 512 * f
            mkeys = [("H2", mm_, si_) for mm_ in range(hr0 // 512, min((hr0 + 513) // 512, 8) + 1) for si_ in range(4 if mm_ < 8 else 1)]
            norm_transpose(lambda r0, ns: H2[hr0 + r0:hr0 + r0 + ns, :], 514, gfull, hnT, 0, nb, "F", src_reads=mkeys)
            if f == 7:
                ts("dve", hnT[:, :, 513:514], hnT[:, :, 513:514], hmask[:, 1:2], None, ALU.mult, None,
                   [HN[0], HN[1], "hmask"], [HN[0], HN[1]])
            dmas("sp", [(h2r[:, si, :], H2[hr0 + 1 + si * 128:hr0 + 1 + (si + 1) * 128, :]) for si in range(4)],
                 mkeys, ["h2r"], "F_h2r")
            for step in range(22):
                ws = wc % 2
                wc += 1
                c0 = step * 256
                dma("sp", wg[ws][:], wb_gate[:, c0:c0 + 256].rearrange("(kc p) c -> p kc c", p=128),
                    wkeys("wb_gate", 0, D), [("wg", ws)], "F_wg%d" % ws)
                dma("sp", wu[ws][:], wb_up[:, c0:c0 + 256].rearrange("(kc p) c -> p kc c", p=128),
                    wkeys("wb_up", 0, D), [("wu", ws)], "F_wu%d" % ws)
                for ff in range(2):
                    fc = step * 2 + ff
                    bg0, bg1, bu = ps_next(), ps_next(), ps_next()
                    for kc in range(16):
                        mm(psum[bg0][:, 0:257], wg[ws][:, kc, ff * 128:(ff + 1) * 128], hnT[:, kc, 0:257], kc == 0,
                           kc == 15, [("wg", ws), HN[kc // 8]], [("ps", bg0)])
                    for kc in range(16):
                        mm(psum[bg1][:, 0:257], wg[ws][:, kc, ff * 128:(ff + 1) * 128], hnT[:, kc, 257:514], kc == 0,
                           kc == 15, [("wg", ws), HN[kc // 8]], [("ps", bg1)])
                    for kc in range(16):
                        mm(psum[bu][:, 0:512], wu[ws][:, kc, ff * 128:(ff + 1) * 128], hnT[:, kc, 1:513], kc == 0,
                           kc == 15, [("wu", ws), HN[kc // 8]], [("ps", bu)])
                    s2 = fc % 2
                    cp("act", gsb[s2][:, 0:257], psum[bg0][:, 0:257], [("ps", bg0)], [("gsb", s2)])
                    cp("act", gsb[s2][:, 257:514], psum[bg1][:, 0:257], [("ps", bg1)], [("gsb", s2)])
                    ts("dve", tcv[s2][:, :], gsb[s2][:, 1:513], convw[:, fc * 3 + 1:fc * 3 + 2], convb[:, fc:fc + 1],
                       ALU.mult, ALU.add, [("gsb", s2), "convw", "convb"], [("tcv", s2)])
                    stt("dve", tcv[s2][:, :], gsb[s2][:, 0:512], convw[:, fc * 3:fc * 3 + 1], tcv[s2][:, :], ALU.mult,
                        ALU.add, [("gsb", s2), "convw", ("tcv", s2)], [("tcv", s2)])
                    stt("dve", tcv[s2][:, :], gsb[s2][:, 2:514], convw[:, fc * 3 + 2:fc * 3 + 3], tcv[s2][:, :],
                        ALU.mult, ALU.add, [("gsb", s2), "convw", ("tcv", s2)], [("tcv", s2)])
                    act(sg[s2][:, :], tcv[s2][:, :], AF.Silu, [("tcv", s2)], [("sg", s2)])
                    tt("dve", aT[:, fc, :], psum[bu][:, 0:512], sg[s2][:, :], ALU.mult, [("ps", bu), ("sg", s2)],
                       ["aT"])
            for cb in range(4):
                acc = [ps_next() for _ in range(4)]
                for part in range(4):
                    ws = wdc % 2
                    wdc += 1
                    dma("sp", wd[ws][:], wb_down[part * 1408:(part + 1) * 1408, cb * 512:(cb + 1) * 512]
                        .rearrange("(fc p) c -> p fc c", p=128), wkeys("wb_down", part * 1408, (part + 1) * 1408),
                        [("wd", ws)], "F_wd%d" % ws)
                    for si in range(4):
                        for j in range(11):
                            fc = part * 11 + j
                            mm(psum[acc[si]][:, :], aT[:, fc, si * 128:(si + 1) * 128], wd[ws][:, j, :],
                               fc == 0, fc == NFC - 1, ["aT", ("wd", ws)], [("ps", acc[si])])
                for si in range(4):
                    tt("dve", h2r[:, si, cb * 512:(cb + 1) * 512], psum[acc[si]][:, :],
                       h2r[:, si, cb * 512:(cb + 1) * 512], ALU.add, [("ps", acc[si]), "h2r"], ["h2r"])
            dmas("pool", [(out[f * 512 + si * 128:f * 512 + (si + 1) * 128, :], h2r[:, si, :]) for si in range(4)],
                 ["h2r"], [("out", f)], "F_out")
    ext = {("dma", k): i for k, i in pg.last_dma.items()}
    pg.add("sp", lambda e: e.nop(), (), [("final",)], extra=ext)
    pg.emit(nc)
    return nc


def _rope_tables(pos):
    inv = (10000.0 ** (-np.arange(0, 64, 2, dtype=np.float32) / np.float32(64))).astype(np.float32)
    ang = pos.astype(np.float32)[None, :] * inv[:, None]
    c = np.cos(ang).astype(np.float32)
    s = np.sin(ang).astype(np.float32)
    return np.concatenate([c, c], 0), np.concatenate([s, s], 0)


def _na_tables(rpb, qtr):
    H = 8
    c = np.arange(64)
    col_start = np.clip(c - 8, 0, 48)
    kc = np.arange(64)
    colv = (kc[:, None] >= col_start[None, :]) & (kc[:, None] < col_start[None, :] + 16)
    dc = np.clip(kc[:, None] - c[None, :] + 15, 0, 30)
    biasI = np.full((128, H, 4, 64), MASKV, np.float32)
    for i in range(4):
        for half in range(2):
            dr = 2 * i + half - 4 + 7
            blk = rpb[:, dr][:, dc]
            blk = np.where(colv[None], blk, MASKV)
            biasI[half * 64:(half + 1) * 64, :, i, :] = blk.transpose(1, 0, 2)
    biasE = np.full((9, 128, H, 7, 64), MASKV, np.float32)
    rls = [-1, 0, 1, 2, 3, 61, 62, 63, 64]
    for ei, rl in enumerate(rls):
        base = -5 if rl <= 3 else 55
        R = 64 * qtr + rl
        if R < 0 or R > 255:
            continue
        rs = min(max(R - 4, 0), 248)
        for i in range(7):
            for half in range(2):
                a = base + 2 * i + half
                A = 64 * qtr + a
                if A < rs or A > rs + 7 or A < 0 or A > 255:
                    continue
                dr = A - R + 7
                blk = rpb[:, dr][:, dc]
                blk = np.where(colv[None], blk, MASKV)
                biasE[ei, half * 64:(half + 1) * 64, :, i, :] = blk.transpose(1, 0, 2)
    return biasI.reshape(128, -1), biasE.reshape(9, 128, -1)


_NC_CACHE = {}


def make_inputs(core, x, meta_tokens, mix_norm_g, w_in, na_q_g, na_k_g, na_rpb, na_meta_bias, mla_cq_g, mla_ckv_g,
                w_q_up, w_kv_up, mla_q_g, mla_k_g, na_out_g, mla_out_g, w_out, ffn_norm_g, w_gate, w_up,
                conv_w, conv_b, w_down, shared):
    b, qtr = core // 4, core % 4
    T0 = qtr * 4096
    f32 = np.float32
    key = ("hseq", b)
    if key not in shared:
        hs = np.zeros((LP, D), f32)
        hs[:16] = meta_tokens
        hs[16:L] = x[b]
        shared[key] = hs
    hext = np.zeros((EXT, D), f32)
    lo = T0 - 320
    a0 = max(lo, 0)
    a1 = min(lo + EXTG, SEQ)
    hext[a0 - lo:a1 - lo] = x[b, a0:a1]
    if qtr == 0:
        hext[319] = meta_tokens[15]
    hext[EXTG:EXT] = meta_tokens
    if "common" not in shared:
        cm = {}
        cm["ident"] = np.eye(128, dtype=f32)
        cm["gmix"] = np.ascontiguousarray(np.broadcast_to(mix_norm_g[0][None, :], (128, D))).astype(f32)
        cm["gffn"] = np.ascontiguousarray(np.broadcast_to(ffn_norm_g[0][None, :], (128, D))).astype(f32)
        gv = np.zeros((128, 64), f32)
        gv[:, 0] = na_q_g[0]
        gv[:, 1] = na_k_g[0]
        gv[:, 2] = mla_q_g[0][:128]
        gv[:64, 3] = mla_q_g[0][128:192]
        gv[:64, 4] = np.concatenate([mla_q_g[0][160:192], mla_q_g[0][128:160]])
        gv[:, 5] = mla_k_g[0][:128]
        gv[:64, 6] = mla_k_g[0][128:192]
        gv[:64, 7] = np.concatenate([mla_k_g[0][160:192], mla_k_g[0][128:160]])
        gv[:, 8:12] = mla_cq_g[0].reshape(4, 128).T
        gv[:, 12:14] = mla_ckv_g[0].reshape(2, 128).T
        gv[:, 16:24] = na_out_g[0].reshape(8, 128).T
        gv[:, 24:32] = mla_out_g[0].reshape(8, 128).T
        cm["gvec"] = gv
        cm["convw"] = np.ascontiguousarray(conv_w[0].reshape(3, NFC, 128).transpose(2, 1, 0)).reshape(128, NFC * 3)
        cm["convb"] = np.ascontiguousarray(conv_b[0].reshape(NFC, 128).T)
        cm["metab"] = np.ascontiguousarray(
            np.broadcast_to(na_meta_bias[0].T[:, :, None], (16, 8, 64))).reshape(16, 512).astype(f32)
        cA, sA = _rope_tables(np.arange(LP))
        cm["cosA"], cm["sinA"] = cA, sA
        for nm, w in (("w_in", w_in), ("w_q_up", w_q_up), ("w_kv_up", w_kv_up), ("w_out", w_out), ("w_gate", w_gate),
                      ("w_up", w_up), ("w_down", w_down)):
            cm[nm] = np.ascontiguousarray(w[0])
        shared["common"] = cm
    cm = shared["common"]
    pos = np.clip(16 + T0 + (np.arange(EXT) - 320), 0, None)
    cB, sB = _rope_tables(pos)
    bI, bE = _na_tables(np.asarray(na_rpb[0]), qtr)
    hm = np.ones((128, 2), f32)
    if qtr == 3:
        hm[:, 1] = 0.0
    d = dict(cm)
    d.update(hseq=shared[("hseq", b)], hext=hext, cosB=cB, sinB=sB, biasI=bI, biasE=bE, hmask=hm)
    return d


def kernel(**inputs):
    inputs = {k: np.asarray(v) for k, v in inputs.items()}
    if "nc" not in _NC_CACHE:
        _NC_CACHE["nc"] = build_program()
    nc = _NC_CACHE["nc"]
    shared = {}
    in_maps = [make_inputs(c, shared=shared, **inputs) for c in range(8)]
    res = run_bass_kernel_spmd(nc, in_maps, core_ids=list(range(8)))
    outp = np.zeros((2, SEQ, D), np.float32)
    for c in range(8):
        outp[c // 4, (c % 4) * 4096:(c % 4 + 1) * 4096] = res.results[c]["out"]
    return outp
```
